# Optimizing a Trainium2 kernel written in Bass

```python
import jax, jax.numpy as jnp
from jax import lax
import numpy as np

D_MODEL = 1024
BATCH = 4
SEQ = 4096
DEPTH = 4

CHUNK = 64
Q_BLOCK = 128
MIX_WIDTH = D_MODEL

RW_HEADS = 8
RW_HEAD_DIM = 64
RW_WIDTH = RW_HEADS * RW_HEAD_DIM
RW_DECAY_LORA = 64
RW_AAA_LORA = 64
RW_GATE_LORA = 128
RW_GN_EPS = 64e-5

MLA_HEADS = 4
MLA_NOPE_DIM = 64
MLA_ROPE_DIM = 32
MLA_V_DIM = 64
MLA_QK_DIM = MLA_NOPE_DIM + MLA_ROPE_DIM
MLA_Q_RANK = 256
MLA_KV_RANK = 128
MLA_WIDTH = MLA_HEADS * MLA_V_DIM
ROPE_THETA = 10000.0

SB_HEADS = 4
SB_HEAD_DIM = 64
SB_WIDTH = SB_HEADS * SB_HEAD_DIM

FFN_HIDDEN = -(-8 * D_MODEL // (3 * 256)) * 256

NORM_EPS = 1e-6
NEG_INF = -1e30

RW_SPLITS = (RW_WIDTH, RW_WIDTH, RW_WIDTH, RW_DECAY_LORA, RW_AAA_LORA, RW_GATE_LORA)
RW_COLS = sum(RW_SPLITS)
REST_SPLITS = (MLA_Q_RANK, MLA_KV_RANK, MLA_ROPE_DIM, SB_WIDTH, SB_WIDTH, SB_WIDTH)
IN_COLS = RW_COLS + sum(REST_SPLITS)

kernel_name = "hybrid_rwkv7_mla_stickbreak_trunk"


def _split_points(sizes):
    return [int(v) for v in np.cumsum(sizes)[:-1]]


def rms_norm(x, g):
    xf = x.astype(jnp.float32)
    y = xf * lax.rsqrt(jnp.mean(xf * xf, axis=-1, keepdims=True) + NORM_EPS)
    return (y * g.astype(jnp.float32)).astype(x.dtype)


def token_shift(p, mu):
    prev = jnp.pad(p, ((0, 0), (1, 0), (0, 0)))[:, :-1]
    return p + mu * (prev - p)


def to_heads(t, n_heads):
    B, T, C = t.shape
    return t.reshape(B, T, n_heads, C // n_heads).transpose(0, 2, 1, 3)


def from_heads(t):
    B, H, T, d = t.shape
    return t.transpose(0, 2, 1, 3).reshape(B, T, H * d)


def rotary_tables(positions):
    inv_freq = ROPE_THETA ** (-jnp.arange(0, MLA_ROPE_DIM, 2, dtype=jnp.float32) / MLA_ROPE_DIM)
    ang = positions.astype(jnp.float32)[..., None] * inv_freq
    return jnp.cos(ang)[:, :, None, :], jnp.sin(ang)[:, :, None, :]


def apply_rotary(x, cos, sin):
    xf = x.astype(jnp.float32)
    x1, x2 = jnp.split(xf, 2, axis=-1)
    return jnp.concatenate([x1 * cos - x2 * sin, x1 * sin + x2 * cos], axis=-1).astype(x.dtype)


def rwkv7_time_mix(r, k, v, wd, ad, gd, w_up, w0, a_up, a0, g_up, k_k, k_a, r_k, ln_g, ln_b):
    B, T, _ = r.shape
    f32 = jnp.float32
    w = -jax.nn.softplus(-(w0 + jnp.tanh(wd) @ w_up)) - 0.5
    decay = jnp.exp(-jnp.exp(w.astype(f32)))
    a = jax.nn.sigmoid(a0 + ad @ a_up)
    g = jax.nn.sigmoid(gd) @ g_up
    heads = lambda t: t.reshape(B, T, RW_HEADS, RW_HEAD_DIM).astype(f32)
    kk = heads(k * k_k)
    kk = kk * lax.rsqrt(jnp.sum(kk * kk, axis=-1, keepdims=True) + 1e-12)
    k = k * (1 + (a - 1) * k_a)
    rh, kh, vh, wh, ah = heads(r), heads(k), heads(v), heads(decay), heads(a)
    a_vec = -kk
    b_vec = kk * ah

    def step(S, inp):
        r_t, w_t, k_t, v_t, a_t, b_t = inp
        sa = jnp.einsum('bhij,bhj->bhi', S, a_t)
        S = S * w_t[:, :, None, :] + sa[..., None] * b_t[:, :, None, :] + v_t[..., None] * k_t[:, :, None, :]
        return S, jnp.einsum('bhij,bhj->bhi', S, r_t)

    seq = tuple(jnp.swapaxes(t, 0, 1) for t in (rh, wh, kh, vh, a_vec, b_vec))
    S0 = jnp.zeros((B, RW_HEADS, RW_HEAD_DIM, RW_HEAD_DIM), f32)
    _, y = lax.scan(step, S0, seq)
    y = jnp.swapaxes(y, 0, 1)
    mu = jnp.mean(y, axis=-1, keepdims=True)
    var = jnp.mean(jnp.square(y - mu), axis=-1, keepdims=True)
    y = ((y - mu) * lax.rsqrt(var + RW_GN_EPS)).reshape(B, T, RW_WIDTH) * ln_g.astype(f32) + ln_b.astype(f32)
    bonus = jnp.sum(rh * kh * r_k.astype(f32), axis=-1, keepdims=True) * vh
    y = y + bonus.reshape(B, T, RW_WIDTH)
    return (y * g.astype(f32)).astype(r.dtype)


def chunk_causal_softmax_attention(q, k, v, scale):
    B, H, T, dk = q.shape
    nb = T // Q_BLOCK
    qb = jnp.moveaxis(q.reshape(B, H, nb, Q_BLOCK, dk), 2, 0)
    k_chunk = jnp.arange(T) // CHUNK

    def one_block(args):
        q_blk, i = args
        q_chunk = (i * Q_BLOCK + jnp.arange(Q_BLOCK)) // CHUNK
        s = jnp.einsum('bhqd,bhkd->bhqk', q_blk, k).astype(jnp.float32) * scale
        s = jnp.where(k_chunk[None, :] <= q_chunk[:, None], s, NEG_INF)
        p = jax.nn.softmax(s, axis=-1).astype(v.dtype)
        return jnp.einsum('bhqk,bhkd->bhqd', p, v)

    out = lax.map(one_block, (qb, jnp.arange(nb)))
    return jnp.moveaxis(out, 0, 2).reshape(B, H, T, v.shape[-1])


def stick_breaking_attention(q, k, v):
    B, H, T, d = q.shape
    nb = T // Q_BLOCK
    qb = jnp.moveaxis(q.reshape(B, H, nb, Q_BLOCK, d), 2, 0)
    k_pos = jnp.arange(T)
    scale = d ** -0.5

    def one_block(args):
        q_blk, i = args
        q_pos = i * Q_BLOCK + jnp.arange(Q_BLOCK)
        strict = k_pos[None, :] < q_pos[:, None]
        z = jnp.einsum('bhqd,bhkd->bhqk', q_blk, k).astype(jnp.float32) * scale
        log_stay = jnp.where(strict, jax.nn.log_sigmoid(-z), 0.0)
        shifted = jnp.pad(log_stay[..., 1:], ((0, 0), (0, 0), (0, 0), (0, 1)))
        log_after = lax.cumsum(shifted, axis=3, reverse=True)
        w = jnp.where(strict, jnp.exp(jax.nn.log_sigmoid(z) + log_after), 0.0)
        return jnp.einsum('bhqk,bhkd->bhqd', w.astype(v.dtype), v)

    out = lax.map(one_block, (qb, jnp.arange(nb)))
    return jnp.moveaxis(out, 0, 2).reshape(B, H, T, d)


def setup_inputs(seed: int = 0) -> dict:
    key = jax.random.key(seed)
    ks = iter(jax.random.split(key, 32))
    f32 = jnp.float32
    L = DEPTH

    def nrm(shape, scale):
        return jax.random.normal(next(ks), shape, f32) * scale

    def gain(shape):
        return 1.0 + nrm(shape, 0.02)

    x = jax.random.normal(next(ks), (BATCH, SEQ, D_MODEL), f32)
    offsets = jax.random.randint(next(ks), (BATCH, 1), 0, 8192, dtype=jnp.int32)
    positions = offsets + jnp.arange(SEQ, dtype=jnp.int32)[None, :]
    return {
        "x": x,
        "positions": positions,
        "attn_norm_g": gain((L, D_MODEL)),
        "w_in": nrm((L, D_MODEL, IN_COLS), D_MODEL ** -0.5),
        "rw_shift_mu": jax.random.uniform(next(ks), (L, RW_COLS), f32),
        "rw_w_up": nrm((L, RW_DECAY_LORA, RW_WIDTH), 0.5 * RW_DECAY_LORA ** -0.5),
        "rw_w0": jax.random.uniform(next(ks), (L, RW_WIDTH), f32, -6.0, 0.0),
        "rw_a_up": nrm((L, RW_AAA_LORA, RW_WIDTH), RW_AAA_LORA ** -0.5),
        "rw_a0": nrm((L, RW_WIDTH), 0.1),
        "rw_g_up": nrm((L, RW_GATE_LORA, RW_WIDTH), RW_GATE_LORA ** -0.5),
        "rw_k_k": 0.85 + nrm((L, RW_WIDTH), 0.02),
        "rw_k_a": gain((L, RW_WIDTH)),
        "rw_r_k": nrm((L, RW_HEADS, RW_HEAD_DIM), 0.1),
        "rw_ln_g": gain((L, RW_WIDTH)),
        "rw_ln_b": nrm((L, RW_WIDTH), 0.02),
        "mla_cq_norm_g": gain((L, MLA_Q_RANK)),
        "mla_ckv_norm_g": gain((L, MLA_KV_RANK)),
        "mla_w_uq": nrm((L, MLA_Q_RANK, MLA_HEADS * MLA_QK_DIM), MLA_Q_RANK ** -0.5),
        "mla_w_ukv": nrm((L, MLA_KV_RANK, MLA_HEADS * (MLA_NOPE_DIM + MLA_V_DIM)), MLA_KV_RANK ** -0.5),
        "mla_q_norm_g": gain((L, MLA_QK_DIM)),
        "mla_k_norm_g": gain((L, MLA_QK_DIM)),
        "w_o": nrm((L, MIX_WIDTH, D_MODEL), MIX_WIDTH ** -0.5),
        "ffn_norm_g": gain((L, D_MODEL)),
        "ffn_w_gate": nrm((L, D_MODEL, FFN_HIDDEN), D_MODEL ** -0.5),
        "ffn_w_up": nrm((L, D_MODEL, FFN_HIDDEN), D_MODEL ** -0.5),
        "ffn_w_down": nrm((L, FFN_HIDDEN, D_MODEL), FFN_HIDDEN ** -0.5),
    }


def reference(x, positions, attn_norm_g, w_in, rw_shift_mu, rw_w_up, rw_w0, rw_a_up, rw_a0, rw_g_up,
              rw_k_k, rw_k_a, rw_r_k, rw_ln_g, rw_ln_b, mla_cq_norm_g, mla_ckv_norm_g, mla_w_uq, mla_w_ukv,
              mla_q_norm_g, mla_k_norm_g, w_o, ffn_norm_g, ffn_w_gate, ffn_w_up, ffn_w_down):
    B, T, _ = x.shape
    cos, sin = rotary_tables(positions)
    rw_pts = _split_points(RW_SPLITS)
    rest_pts = _split_points(REST_SPLITS)
    for l in range(DEPTH):
        h = rms_norm(x, attn_norm_g[l]) @ w_in[l]
        rw_cols = token_shift(h[..., :RW_COLS], rw_shift_mu[l])
        r, k, v, wd, ad, gd = jnp.split(rw_cols, rw_pts, axis=-1)
        c_q, c_kv, k_rope, sb_q, sb_k, sb_v = jnp.split(h[..., RW_COLS:], rest_pts, axis=-1)

        y_rw = rwkv7_time_mix(r, k, v, wd, ad, gd, rw_w_up[l], rw_w0[l], rw_a_up[l], rw_a0[l], rw_g_up[l],
                              rw_k_k[l], rw_k_a[l], rw_r_k[l], rw_ln_g[l], rw_ln_b[l])

        q = (rms_norm(c_q, mla_cq_norm_g[l]) @ mla_w_uq[l]).reshape(B, T, MLA_HEADS, MLA_QK_DIM)
        kv = (rms_norm(c_kv, mla_ckv_norm_g[l]) @ mla_w_ukv[l]).reshape(B, T, MLA_HEADS, MLA_NOPE_DIM + MLA_V_DIM)
        k_nope, v_mla = jnp.split(kv, [MLA_NOPE_DIM], axis=-1)
        k_full = jnp.concatenate(
            [k_nope, jnp.broadcast_to(k_rope[:, :, None, :], (B, T, MLA_HEADS, MLA_ROPE_DIM))], axis=-1)
        q = rms_norm(q, mla_q_norm_g[l])
        k_full = rms_norm(k_full, mla_k_norm_g[l])
        q = jnp.concatenate([q[..., :MLA_NOPE_DIM], apply_rotary(q[..., MLA_NOPE_DIM:], cos, sin)], axis=-1)
        k_full = jnp.concatenate(
            [k_full[..., :MLA_NOPE_DIM], apply_rotary(k_full[..., MLA_NOPE_DIM:], cos, sin)], axis=-1)
        o_mla = chunk_causal_softmax_attention(
            q.transpose(0, 2, 1, 3), k_full.transpose(0, 2, 1, 3), v_mla.transpose(0, 2, 1, 3),
            MLA_QK_DIM ** -0.5)
        y_mla = from_heads(o_mla)

        o_sb = stick_breaking_attention(to_heads(sb_q, SB_HEADS), to_heads(sb_k, SB_HEADS), to_heads(sb_v, SB_HEADS))
        y_sb = from_heads(o_sb)

        mixed = jnp.concatenate([y_rw, y_mla.astype(y_rw.dtype), y_sb.astype(y_rw.dtype)], axis=-1)
        x = x + (mixed @ w_o[l]).astype(x.dtype)

        hn = rms_norm(x, ffn_norm_g[l])
        x = x + ((jax.nn.silu(hn @ ffn_w_gate[l]) * (hn @ ffn_w_up[l])) @ ffn_w_down[l]).astype(x.dtype)
    return x
```

```python
import contextlib
import math
import numpy as np
import ml_dtypes
import concourse.bass as bass
import concourse.mybir as mybir
from concourse.bass_utils import run_bass_kernel_spmd

F32 = mybir.dt.float32
BF16 = mybir.dt.bfloat16
I32 = mybir.dt.int32
AF = mybir.ActivationFunctionType
ALU = mybir.AluOpType
AX = mybir.AxisListType

D = 1024
IN_COLS = 2976
FF = 2816
NJ = FF // 128
DEC_C = math.exp(-0.5)
ENGS = ("pe", "act", "dve", "pool", "sp")


class Buf:
    __slots__ = ("w", "r")

    def __init__(self):
        self.w = None
        self.r = []


class Tile:
    def __init__(self, t):
        self.t = t
        self.b = Buf()


class Prog:
    def __init__(self, nc, n_dma_sems=24):
        self.nc = nc
        self.ops = {e: [] for e in ENGS}
        self.sems = {}
        self.cnt = {}
        for e in ENGS:
            self.sems[e] = nc.alloc_semaphore(name="s_" + e)
            self.cnt[e] = 0
        self.dma_sems = []
        for i in range(n_dma_sems):
            k = "dma%d" % i
            self.sems[k] = nc.alloc_semaphore(name="s_" + k)
            self.dma_sems.append(k)
            self.cnt[k] = 0
        self.dma_rr = 0
        self.seen = {e: {} for e in ENGS}
        import os
        self.gidx = 0
        self.maxops = int(os.environ.get("MAXOPS", "1000000000"))
        self.last_desc = None

    def _deps(self, eng, reads, writes):
        deps = {}

        def add(ev):
            if ev is not None and deps.get(ev[0], 0) < ev[1]:
                deps[ev[0]] = ev[1]
        for b in reads:
            add(b.w)
        for b in writes:
            add(b.w)
            for ev in b.r:
                add(ev)
        waits = []
        for k, v in deps.items():
            if k == eng and eng in ("pe", "sp"):
                continue
            if self.seen[eng].get(k, 0) >= v:
                continue
            self.seen[eng][k] = v
            waits.append((k, v))
        return waits

    def _commit(self, ev, reads, writes):
        for b in reads:
            b.r.append(ev)
            if len(b.r) > 48:
                m = {}
                for k, v in b.r:
                    if m.get(k, 0) < v:
                        m[k] = v
                b.r = list(m.items())
        for b in writes:
            b.w = ev
            b.r = []

    def op(self, eng, fn, reads=(), writes=(), desc=None):
        self.gidx += 1
        if self.gidx > self.maxops:
            return None
        self.last_desc = (self.gidx, eng, desc)
        reads = [t.b for t in reads]
        writes = [t.b for t in writes]
        waits = self._deps(eng, reads, writes)
        self.cnt[eng] += 1
        ev = (eng, self.cnt[eng])
        self.ops[eng].append((waits, fn, (eng, 1)))
        self._commit(ev, reads, writes)
        return ev

    def dma(self, fn, reads=(), writes=(), q="sp", desc=None):
        self.gidx += 1
        if self.gidx > self.maxops:
            return None
        self.last_desc = (self.gidx, "dma", desc)
        reads = [t.b for t in reads]
        writes = [t.b for t in writes]
        k = self.dma_sems[self.dma_rr]
        self.dma_rr = (self.dma_rr + 1) % len(self.dma_sems)
        waits = self._deps(q, reads, writes)
        prev = self.cnt[k]
        if prev > 0 and self.seen[q].get(k, 0) < prev:
            self.seen[q][k] = prev
            waits.append((k, prev))
        self.cnt[k] += 16
        ev = (k, self.cnt[k])
        self.ops[q].append((waits, fn, (k, 16)))
        self._commit(ev, reads, writes)
        return ev

    def barrier(self):
        for e in ENGS:
            waits = []
            for k, v in self.cnt.items():
                if v > 0 and self.seen[e].get(k, 0) < v and not (k == e and e in ("sp",)):
                    self.seen[e][k] = v
                    waits.append((k, v))
            if waits:
                self.ops[e].append((waits, None, None))

    def emit(self):
        sems = self.sems
        ops = self.ops

        def replay(lst, eng):
            for waits, fn, inc in lst:
                for k, v in waits:
                    eng.wait_ge(sems[k], v)
                if fn is not None:
                    fn(eng).then_inc(sems[inc[0]], inc[1])

        with self.nc.Block() as block:
            @block.tensor
            def _(e):
                replay(ops["pe"], e)

            @block.scalar
            def _(e):
                replay(ops["act"], e)

            @block.vector
            def _(e):
                replay(ops["dve"], e)

            @block.gpsimd
            def _(e):
                replay(ops["pool"], e)

            @block.sync
            def _(e):
                replay(ops["sp"], e)


def run_interleaved(gens):
    gens = list(gens)
    while gens:
        for g in list(gens):
            try:
                next(g)
            except StopIteration:
                gens.remove(g)


class Builder:
    def __init__(self, T, L, dbg=False):
        self.T, self.L, self.dbg = T, L, dbg
        self.nc = nc = bass.Bass("TRN2", target_bir_lowering=False)
        self.p = Prog(nc)
        self.NT = T // 512
        self.NB = T // 128
        self._rot = {}
        self.din = {}
        self.uid = 0
        self.dumped = set()
        self.padded = []
        import os
        self.pad128 = os.environ.get("PAD128", "0") == "1"

    def inp(self, name, shape, dt=F32):
        ap = self.nc.dram_tensor(name, list(shape), dt, kind="ExternalInput").ap()
        self.din[name] = ap
        return ap

    def scratch(self, name, shape, dt):
        kind = "ExternalOutput" if self.dbg else "Internal"
        return self.nc.dram_tensor(name, list(shape), dt, kind=kind).ap()

    def S(self, st, name, shape, dt):
        self.uid += 1
        shape = list(shape)
        if shape[0] < 128 and self.pad128:
            p0 = shape[0]
            h = st.enter_context(self.nc.sbuf_tensor("%s_%d" % (name, self.uid), [128] + shape[1:], dt))
            t = Tile(h[0:p0])
            t.full = h
            self.padded.append(t)
            return t
        return Tile(st.enter_context(self.nc.sbuf_tensor("%s_%d" % (name, self.uid), shape, dt)))

    def R(self, st, name, shape, dt, n):
        return [self.S(st, name + str(i), shape, dt) for i in range(n)]

    def nxt(self, lst):
        i = self._rot.get(id(lst), 0)
        self._rot[id(lst)] = i + 1
        return lst[i % len(lst)]

    def I(self, eng, m, r, w, **kw):
        return self.p.op(eng, lambda e: getattr(e, m)(**kw), r, w, desc=(m, {k: str(v)[:150] for k, v in kw.items()}))

    def Dm(self, r, w, out, in_, q="sp", **kw):
        return self.p.dma(lambda e: e.dma_start(out=out, in_=in_, **kw), r, w, q=q)

    def dump(self, name, tile, ap):
        if not self.dbg or name in self.dumped:
            return
        self.dumped.add(name)
        d = self.nc.dram_tensor("dbg_" + name, list(ap.shape), ap.dtype, kind="ExternalOutput").ap()
        self.Dm([tile], [], d, ap)

    def ps(self):
        return self.nxt(self.psf)

    def pt(self):
        return self.nxt(self.psb)

    def build(self):
        nc, p, T, L, NT, NB = self.nc, self.p, self.T, self.L, self.NT, self.NB
        I, Dm, S, R, nxt = self.I, self.Dm, self.S, self.R, self.nxt
        inp = self.inp
        x_in = inp("x", [T, D])
        pos_in = inp("pos", [128, NB], I32)
        w_in = inp("w_in", [L, D, IN_COLS])
        w_o = inp("w_o", [L, D, D])
        wg = inp("wg", [L, D, FF])
        wu = inp("wu", [L, D, FF])
        wd = inp("wd", [L, FF, D])
        gA = inp("gA", [L, 128, D])
        gF = inp("gF", [L, 128, D])
        mu = inp("mu", [L, 128, 14])
        hpp = inp("hpp", [L, 128, 6, 4])
        wa_up = inp("wa_up", [L, 128, 512])
        g_up = inp("g_up", [L, 128, 512])
        lng = inp("lng", [L, 128, 512])
        lnb = inp("lnb", [L, 128, 512])
        gcq = inp("gcq", [L, 128, 256])
        gckv = inp("gckv", [L, 128, 128])
        gqn = inp("gqn", [L, 128, 96])
        gkn = inp("gkn", [L, 128, 96])
        w_uq = inp("w_uq", [L, 256, 384])
        w_ukv = inp("w_ukv", [L, 128, 512])
        c_ident = inp("c_ident", [128, 128])
        c_masks = inp("c_masks", [64, 4, 512])
        c_cmask = inp("c_cmask", [128, 512])
        c_blk = inp("c_blk", [128, 128])
        c_blk2 = inp("c_blk2", [128, 2])
        c_tri = inp("c_tri", [128, 128])
        c_sbm = inp("c_sbm", [128, 128])
        c_mlam = inp("c_mlam", [128, 128])
        c_invf = inp("c_invf", [128, 16])
        out = nc.dram_tensor("out", [T, D], F32, kind="ExternalOutput").ap()

        mixed = self.scratch("mixed", [T, D], BF16)
        hrw = self.scratch("hrw", [14, 128, T], F32)
        sbqT = self.scratch("sbqT", [256, T], BF16)
        sbkT = self.scratch("sbkT", [256, T], BF16)
        sbv = self.scratch("sbv", [T, 256], BF16)
        mqT = self.scratch("mqT", [384, T], BF16)
        mkT = self.scratch("mkT", [384, T], BF16)
        mv = self.scratch("mv", [T, 256], BF16)
        self.wgs = self.nc.dram_tensor("wgs", [NJ, 128, 1024], BF16, kind="Internal").ap()
        self.wus = self.nc.dram_tensor("wus", [NJ, 128, 1024], BF16, kind="Internal").ap()
        self.wds = self.nc.dram_tensor("wds", [NJ, 128, 1024], BF16, kind="Internal").ap()

        self.psf = [Tile(nc.alloc_psum_tensor("psf%d" % i, [128, 512], F32)) for i in range(6)]
        self.psb = [Tile(nc.alloc_psum_tensor("psb%d" % i, [128, 1024], BF16)) for i in range(2)]
        self.psb_f32 = []
        for t_ in self.psb:
            v = Tile(t_.t.bitcast(F32))
            v.b = t_.b
            self.psb_f32.append(v)

        with contextlib.ExitStack() as gst:
            identf = S(gst, "identf", [128, 128], F32)
            ident = S(gst, "ident", [128, 128], BF16)
            cosT = S(gst, "cosT", [128, NB, 16], F32)
            sinT = S(gst, "sinT", [128, NB, 16], F32)
            Dm([], [identf], identf.t[:], c_ident)
            I("dve", "tensor_copy", [identf], [ident], out=ident.t[:], in_=identf.t[:])
            self.ident, self.identf = ident, identf
            self.rotary_tables(gst, pos_in, c_invf, cosT, sinT)
            p.barrier()
            import os
            PH = os.environ.get("KPH", "1r23f")
            for l in range(L):
                xcur = x_in if l == 0 else out
                if "1" in PH:
                  self.phase1a(l, xcur, w_in, gA, mu, gcq, gckv, gqn, gkn, w_uq, w_ukv, cosT, sinT,
                             hrw, sbqT, sbkT, sbv, mqT, mkT, mv)
                p.barrier()
                if "r" in PH:
                  self.phase_rwkv(l, hrw, mixed, hpp, wa_up, g_up, lng, lnb, c_masks, c_cmask, c_blk, c_blk2)
                p.barrier()
                if "2" in PH:
                  self.phase_sb(l, sbqT, sbkT, sbv, mixed, 768, c_tri, c_sbm)
                p.barrier()
                if "3" in PH:
                  with contextlib.ExitStack() as cst:
                    self.ffn_conv_setup(cst)
                    run_interleaved([self.phase_attn_gen(l, "mla", mqT, mkT, mv, mixed, 512, c_tri, c_mlam),
                                     self.ffn_conv_gen(l, wg, wu, wd)])
                elif "f" in PH:
                  with contextlib.ExitStack() as cst:
                    self.ffn_conv_setup(cst)
                    run_interleaved([self.ffn_conv_gen(l, wg, wu, wd)])
                p.barrier()
                if "f" in PH:
                  self.phase_ffn(l, xcur, out, mixed, w_o, gF, wg, wu, wd)
                p.barrier()
        p.emit()
        return nc

    def rotary_tables(self, gst, pos_in, c_invf, cosT, sinT):
        I, Dm, S = self.I, self.Dm, self.S
        NB = self.NB
        with contextlib.ExitStack() as st:
            posi = S(st, "posi", [128, NB], I32)
            posf = S(st, "posf", [128, NB], F32)
            invf = S(st, "invf", [128, 16], F32)
            ang = S(st, "ang", [128, NB, 16], F32)
            t1 = S(st, "rt1", [128, NB, 16], F32)
            ni = S(st, "rni", [128, NB, 16], I32)
            nf = S(st, "rnf", [128, NB, 16], F32)
            Dm([], [posi], posi.t[:], pos_in)
            Dm([], [invf], invf.t[:], c_invf)
            I("dve", "tensor_copy", [posi], [posf], out=posf.t[:], in_=posi.t[:])
            I("dve", "tensor_tensor", [posf, invf], [ang], out=ang.t[:],
              in0=posf.t[:].unsqueeze(2).broadcast_to([128, NB, 16]),
              in1=invf.t[:].unsqueeze(1).broadcast_to([128, NB, 16]), op=ALU.mult)
            C1 = 6.28125
            C2 = 2 * math.pi - C1
            for shift, dst in ((0.0, sinT), (math.pi / 2, cosT)):
                I("dve", "tensor_scalar", [ang], [t1], out=t1.t[:], in0=ang.t[:], scalar1=shift, scalar2=None, op0=ALU.add)
                I("dve", "tensor_scalar", [t1], [ni], out=ni.t[:], in0=t1.t[:], scalar1=1.0 / (2 * math.pi), scalar2=None, op0=ALU.mult)
                I("dve", "tensor_copy", [ni], [nf], out=nf.t[:], in_=ni.t[:])
                I("dve", "scalar_tensor_tensor", [nf, t1], [t1], out=t1.t[:], in0=nf.t[:], scalar=-C1, in1=t1.t[:], op0=ALU.mult, op1=ALU.add)
                I("dve", "scalar_tensor_tensor", [nf, t1], [t1], out=t1.t[:], in0=nf.t[:], scalar=-C2, in1=t1.t[:], op0=ALU.mult, op1=ALU.add)
                I("dve", "tensor_scalar", [t1], [nf], out=nf.t[:], in0=t1.t[:], scalar1=math.pi, scalar2=-2 * math.pi, op0=ALU.is_gt, op1=ALU.mult)
                I("dve", "tensor_tensor", [t1, nf], [t1], out=t1.t[:], in0=t1.t[:], in1=nf.t[:], op=ALU.add)
                I("dve", "tensor_scalar", [t1], [nf], out=nf.t[:], in0=t1.t[:], scalar1=-math.pi, scalar2=2 * math.pi, op0=ALU.is_lt, op1=ALU.mult)
                I("dve", "tensor_tensor", [t1, nf], [t1], out=t1.t[:], in0=t1.t[:], in1=nf.t[:], op=ALU.add)
                I("dve", "tensor_scalar", [t1], [t1], out=t1.t[:], in0=t1.t[:], scalar1=3.14159, scalar2=-3.14159, op0=ALU.min, op1=ALU.max)
                I("act", "activation", [t1], [dst], out=dst.t[:], in_=t1.t[:], func=AF.Sin)
            self.p.barrier()

    def rms_rows(self, st_tiles, src_ap, src_tile, width, eps, out_ap, out_tile, gain_ap=None, gain_tile=None, eng2="dve"):
        I, nxt = self.I, self.nxt
        junk, ssl, rsl = st_tiles
        jk = nxt(junk)
        ss = nxt(ssl)
        rs = nxt(rsl)
        I("act", "activation", [src_tile], [jk, ss], out=jk.t[:, 0:width], in_=src_ap, func=AF.Square, accum_out=ss.t[:, 0:1])
        I("act", "activation", [ss], [rs], out=rs.t[:, 0:1], in_=ss.t[:, 0:1], func=AF.Sqrt, scale=1.0 / width, bias=eps)
        I("dve", "reciprocal", [rs], [rs], out=rs.t[:, 0:1], in_=rs.t[:, 0:1])
        if gain_ap is None:
            I(eng2, "tensor_scalar", [src_tile, rs], [out_tile], out=out_ap, in0=src_ap, scalar1=rs.t[:, 0:1], scalar2=None, op0=ALU.mult)
        else:
            I("dve", "scalar_tensor_tensor", [src_tile, rs, gain_tile], [out_tile], out=out_ap, in0=src_ap, scalar=rs.t[:, 0:1],
              in1=gain_ap, op0=ALU.mult, op1=ALU.mult)
        return rs

    def transpose_to(self, src_tile, src_aps, dst_tile, dst_ap_fn, nparts_in, evac_eng="dve"):
        I = self.I
        pt = self.pt()
        for i, a in enumerate(src_aps):
            I("pe", "transpose", [src_tile, self.ident], [pt], out=pt.t[0:a.shape[1], i * 128:i * 128 + nparts_in], in_=a,
              identity=self.ident.t[0:nparts_in, 0:nparts_in])
        o, i_ = dst_ap_fn(pt.t)
        I(evac_eng, "tensor_copy", [pt], [dst_tile], out=o, in_=i_)

    def phase1a(self, l, xcur, w_in, gA, mu, gcq, gckv, gqn, gkn, w_uq, w_ukv, cosT, sinT,
                hrw, sbqT, sbkT, sbv, mqT, mkT, mv):
        I, Dm, S, R, nxt, p = self.I, self.Dm, self.S, self.R, self.nxt, self.p
        T, NT = self.T, self.NT
        with contextlib.ExitStack() as st:
            Win = S(st, "Win", [128, 8, IN_COLS], BF16)
            if True:
                stg = R(st, "stg", [128, IN_COLS], F32, 2)
                import os
                P1 = os.environ.get("P1SKIP", "")
                for k in range(8):
                    if "win" in P1:
                        break
                    s = nxt(stg)
                    Dm([], [s], s.t[:], w_in[l, k * 128:(k + 1) * 128, :])
                    if "cast" in P1:
                        continue
                    I(("dve" if "nopool" in P1 else "pool") if k % 2 else "dve", "tensor_copy", [s], [Win], out=Win.t[:, k, :], in_=s.t[:])
            if "rest" in P1:
                return
            gAt = S(st, "gAt", [128, D], F32)
            mut = S(st, "mut", [128, 14], F32)
            gcqt = S(st, "gcqt", [128, 256], F32)
            gckvt = S(st, "gckvt", [128, 128], F32)
            gqnt = S(st, "gqnt", [128, 4, 96], F32)
            gknt = S(st, "gknt", [128, 4, 96], F32)
            wuqf = S(st, "wuqf", [128, 2, 384], F32)
            wuq = S(st, "wuq", [128, 2, 384], BF16)
            wukvf = S(st, "wukvf", [128, 512], F32)
            wukv = S(st, "wukv", [128, 512], BF16)
            Dm([], [gAt], gAt.t[:], gA[l])
            Dm([], [mut], mut.t[:], mu[l])
            Dm([], [gcqt], gcqt.t[:], gcq[l])
            Dm([], [gckvt], gckvt.t[:], gckv[l])
            for h in range(4):
                Dm([], [gqnt], gqnt.t[:, h, :], gqn[l])
                Dm([], [gknt], gknt.t[:, h, :], gkn[l])
            Dm([], [wuqf], wuqf.t[:], w_uq[l].rearrange("(k p) c -> p k c", p=128))
            Dm([], [wukvf], wukvf.t[:], w_ukv[l])
            I("dve", "tensor_copy", [wuqf], [wuq], out=wuq.t[:], in_=wuqf.t[:])
            I("dve", "tensor_copy", [wukvf], [wukv], out=wukv.t[:], in_=wukvf.t[:])
            I("dve", "tensor_scalar", [gqnt], [gqnt], out=gqnt.t[:], in0=gqnt.t[:], scalar1=96.0 ** -0.5, scalar2=None, op0=ALU.mult)

            xs = R(st, "xs", [128, D], F32, 2)
            junk = R(st, "junk", [128, D], BF16, 2)
            ssl = R(st, "ss", [128, 1], F32, 4)
            rsl = R(st, "rs", [128, 1], F32, 4)
            stt = (junk, ssl, rsl)
            xnb = R(st, "xnb", [128, D], BF16, 2)
            xnT = R(st, "xnT", [128, 8, 512], BF16, 2)
            hlast = S(st, "hlast", [128, 14], F32)
            hraw = R(st, "hraw", [128, 513], F32, 3)
            hdiff = R(st, "hdiff", [128, 512], F32, 2)
            hs = R(st, "hs", [128, 512], F32, 3)
            qkb = R(st, "qkb", [128, 512], BF16, 3)
            vtb = R(st, "vtb", [128, 256], BF16, 2)
            cqn = R(st, "cqn", [128, 256], BF16, 2)
            latf = R(st, "latf", [128, 416], F32, 2)
            cqT = R(st, "cqT", [128, 2, 128], BF16, 2)
            ckn = R(st, "ckn", [128, 128], BF16, 2)
            ckT = R(st, "ckT", [128, 128], BF16, 2)
            krp = R(st, "krp", [128, 32], F32, 2)
            qf = R(st, "qf", [128, 4, 96], F32, 2)
            kf = R(st, "kf", [128, 4, 96], F32, 2)
            sq4 = R(st, "sq4", [128, 4, 96], F32, 2)
            st4 = R(st, "st4", [128, 4], F32, 4)
            rt1 = R(st, "rt1", [128, 4, 16], F32, 4)
            qb = R(st, "qb", [128, 4, 96], BF16, 2)
            kb = R(st, "kb", [128, 4, 96], BF16, 2)
            qTs = R(st, "qTs", [96, 4, 128], BF16, 2)
            kTs = R(st, "kTs", [96, 4, 128], BF16, 2)
            vmb = R(st, "vmb", [128, 4, 64], BF16, 2)
            I("dve", "memset", [], [hlast], ap=hlast.t[:], constant=0.0)

            import os
            KS = os.environ.get("KS", "z")
            if KS < "b":
                return
            for ti in range(NT):
                xt_ = nxt(xnT)
                for j in range(4):
                    r0 = ti * 512 + j * 128
                    xsj = nxt(xs)
                    Dm([], [xsj], xsj.t[:], xcur[r0:r0 + 128, :])
                    xb = nxt(xnb)
                    self.rms_rows(stt, xsj.t[:], xsj, D, 1e-6, xb.t[:], xb, gAt.t[:], gAt)
                    self.transpose_to(xb, [xb.t[:, k * 128:(k + 1) * 128] for k in range(8)], xt_,
                                      lambda pv, j=j, xt_=xt_: (xt_.t[:, :, j * 128:(j + 1) * 128],
                                                               pv[:, :].rearrange("p (k c) -> p k c", c=128)), 128)
                if KS < "c":
                    continue
                for c in range(14):
                    ps = self.ps()
                    for k in range(8):
                        I("pe", "matmul", [Win, xt_], [ps], out=ps.t[:, :], lhsT=Win.t[:, k, c * 128:(c + 1) * 128],
                          rhs=xt_.t[:, k, :], start=(k == 0), stop=(k == 7))
                    hr = nxt(hraw)
                    I("pool", "tensor_copy", [hlast], [hr], out=hr.t[:, 0:1], in_=hlast.t[:, c:c + 1])
                    I("act", "activation", [ps], [hr], out=hr.t[:, 1:513], in_=ps.t[:, :], func=AF.Copy)
                    I("pool", "tensor_copy", [hr], [hlast], out=hlast.t[:, c:c + 1], in_=hr.t[:, 512:513])
                    hd = nxt(hdiff)
                    I("pool", "tensor_tensor", [hr], [hd], out=hd.t[:], in0=hr.t[:, 0:512], in1=hr.t[:, 1:513], op=ALU.subtract)
                    ho = nxt(hs)
                    I("dve", "scalar_tensor_tensor", [hd, mut, hr], [ho], out=ho.t[:], in0=hd.t[:], scalar=mut.t[:, c:c + 1],
                      in1=hr.t[:, 1:513], op0=ALU.mult, op1=ALU.add)
                    Dm([ho], [], hrw[c, :, ti * 512:(ti + 1) * 512], ho.t[:])
                for c in range(4):
                    ps = self.ps()
                    c0 = 2208 + c * 128
                    for k in range(8):
                        I("pe", "matmul", [Win, xt_], [ps], out=ps.t[:, :], lhsT=Win.t[:, k, c0:c0 + 128],
                          rhs=xt_.t[:, k, :], start=(k == 0), stop=(k == 7))
                    qk = nxt(qkb)
                    I("act", "activation", [ps], [qk], out=qk.t[:], in_=ps.t[:, :], func=AF.Copy, scale=(0.125 if c < 2 else 1.0))
                    dst = sbqT if c < 2 else sbkT
                    cc = c % 2
                    Dm([qk], [], dst[cc * 128:(cc + 1) * 128, ti * 512:(ti + 1) * 512], qk.t[:])
                if KS < "d":
                    continue
                for j in range(4):
                    r0 = ti * 512 + j * 128
                    jb = ti * 4 + j
                    ps = self.ps()
                    for k in range(8):
                        I("pe", "matmul", [Win, xt_], [ps], out=ps.t[:, 0:256], lhsT=xt_.t[:, k, j * 128:(j + 1) * 128],
                          rhs=Win.t[:, k, 2720:2976], start=(k == 0), stop=(k == 7))
                    vt = nxt(vtb)
                    I("act", "activation", [ps], [vt], out=vt.t[:], in_=ps.t[:, 0:256], func=AF.Copy)
                    Dm([vt], [], sbv[r0:r0 + 128, :], vt.t[:])
                    ps = self.ps()
                    for k in range(8):
                        I("pe", "matmul", [Win, xt_], [ps], out=ps.t[:, 0:416], lhsT=xt_.t[:, k, j * 128:(j + 1) * 128],
                          rhs=Win.t[:, k, 1792:2208], start=(k == 0), stop=(k == 7))
                    if KS < "e":
                        continue
                    lat = nxt(latf)
                    I("act", "activation", [ps], [lat], out=lat.t[:], in_=ps.t[:, 0:416], func=AF.Copy)
                    cq = nxt(cqn)
                    self.rms_rows(stt, lat.t[:, 0:256], lat, 256, 1e-6, cq.t[:], cq, gcqt.t[:], gcqt)
                    ck = nxt(ckn)
                    self.rms_rows(stt, lat.t[:, 256:384], lat, 128, 1e-6, ck.t[:], ck, gckvt.t[:], gckvt)
                    kr = nxt(krp)
                    I("pool", "tensor_copy", [lat], [kr], out=kr.t[:], in_=lat.t[:, 384:416])
                    if KS < "e2":
                        continue
                    cT = nxt(cqT)
                    self.transpose_to(cq, [cq.t[:, 0:128], cq.t[:, 128:256]], cT,
                                      lambda pv, cT=cT: (cT.t[:], pv[:, 0:256].rearrange("p (k c) -> p k c", c=128)), 128)
                    kT_ = nxt(ckT)
                    self.transpose_to(ck, [ck.t[:, :]], kT_, lambda pv, kT_=kT_: (kT_.t[:], pv[:, 0:128]), 128)
                    if KS < "e3":
                        continue
                    psq = self.ps()
                    for k in range(2):
                        I("pe", "matmul", [cT, wuq], [psq], out=psq.t[:, 0:384], lhsT=cT.t[:, k, :], rhs=wuq.t[:, k, :],
                          start=(k == 0), stop=(k == 1))
                    pskv = self.ps()
                    I("pe", "matmul", [kT_, wukv], [pskv], out=pskv.t[:, :], lhsT=kT_.t[:], rhs=wukv.t[:], start=True, stop=True)
                    if KS < "f":
                        continue
                    q_ = nxt(qf)
                    k_ = nxt(kf)
                    I("act", "activation", [psq], [q_], out=q_.t[:], in_=psq.t[:, 0:384].rearrange("p (h d) -> p h d", d=96), func=AF.Copy)
                    kvv = pskv.t[:, :].rearrange("p (h d) -> p h d", d=128)
                    I("act", "activation", [pskv], [k_], out=k_.t[:, :, 0:64], in_=kvv[:, :, 0:64], func=AF.Copy)
                    I("pool", "tensor_copy", [kr], [k_], out=k_.t[:, :, 64:96], in_=kr.t[:].unsqueeze(1).broadcast_to([128, 4, 32]))
                    vm = nxt(vmb)
                    I("act", "activation", [pskv], [vm], out=vm.t[:], in_=kvv[:, :, 64:128], func=AF.Copy)
                    Dm([vm], [], mv[r0:r0 + 128, :], vm.t[:].rearrange("p h d -> p (h d)"))
                    if KS < "g":
                        continue
                    for (src, gn, dstb, dstT, dram) in ((q_, gqnt, qb, qTs, mqT), (k_, gknt, kb, kTs, mkT)):
                        sq = nxt(sq4)
                        I("pool", "tensor_tensor", [src], [sq], out=sq.t[:], in0=src.t[:], in1=src.t[:], op=ALU.mult)
                        s4 = nxt(st4)
                        I("dve", "tensor_reduce", [sq], [s4], out=s4.t[:], in_=sq.t[:], axis=AX.X, op=ALU.add)
                        I("act", "activation", [s4], [s4], out=s4.t[:], in_=s4.t[:], func=AF.Sqrt, scale=1.0 / 96, bias=1e-6)
                        I("dve", "reciprocal", [s4], [s4], out=s4.t[:], in_=s4.t[:])
                        I("dve", "tensor_tensor", [src, s4], [src], out=src.t[:], in0=src.t[:],
                          in1=s4.t[:].unsqueeze(2).broadcast_to([128, 4, 96]), op=ALU.mult)
                        I("dve", "tensor_tensor", [src, gn], [src], out=src.t[:], in0=src.t[:], in1=gn.t[:], op=ALU.mult)
                        db = nxt(dstb)
                        I("act", "activation", [src], [db], out=db.t[:, :, 0:64], in_=src.t[:, :, 0:64], func=AF.Copy)
                        cosb = cosT.t[:, jb, :].unsqueeze(1).broadcast_to([128, 4, 16])
                        sinb = sinT.t[:, jb, :].unsqueeze(1).broadcast_to([128, 4, 16])
                        a1 = nxt(rt1); a2 = nxt(rt1)
                        I("dve", "tensor_tensor", [src, cosT], [a1], out=a1.t[:], in0=src.t[:, :, 64:80], in1=cosb, op=ALU.mult)
                        I("pool", "tensor_tensor", [src, sinT], [a2], out=a2.t[:], in0=src.t[:, :, 80:96], in1=sinb, op=ALU.mult)
                        I("dve", "tensor_tensor", [a1, a2], [db], out=db.t[:, :, 64:80], in0=a1.t[:], in1=a2.t[:], op=ALU.subtract)
                        a3 = nxt(rt1); a4 = nxt(rt1)
                        I("dve", "tensor_tensor", [src, sinT], [a3], out=a3.t[:], in0=src.t[:, :, 64:80], in1=sinb, op=ALU.mult)
                        I("pool", "tensor_tensor", [src, cosT], [a4], out=a4.t[:], in0=src.t[:, :, 80:96], in1=cosb, op=ALU.mult)
                        I("dve", "tensor_tensor", [a3, a4], [db], out=db.t[:, :, 80:96], in0=a3.t[:], in1=a4.t[:], op=ALU.add)
                        dT = nxt(dstT)
                        self.transpose_to(db, [db.t[:, h, :] for h in range(4)], dT,
                                          lambda pv, dT=dT: (dT.t[:], pv[0:96, 0:512].rearrange("p (h c) -> p h c", c=128)), 128)
                        Dm([dT], [], dram.rearrange("(h d) t -> d h t", d=96)[:, :, r0:r0 + 128], dT.t[:])

    def phase_rwkv(self, l, hrw, mixed, hpp, wa_up, g_up, lng, lnb, c_masks, c_cmask, c_blk, c_blk2):
        import os
        I, Dm, S, R, nxt, p = self.I, self.Dm, self.S, self.R, self.nxt, self.p
        self.padded = []
        T, NT = self.T, self.NT
        c = DEC_C
        with contextlib.ExitStack() as st:
            mk = S(st, "mk", [64, 4, 512], F32)
            mkb = S(st, "mkb", [64, 4, 512], BF16)
            cmask = S(st, "cmask", [128, 512], F32)
            blk = S(st, "blk", [128, 128], F32)
            blk2 = S(st, "blk2", [128, 2], F32)
            hp_ = S(st, "hp", [128, 6, 4], F32)
            waf = S(st, "waf", [128, 512], F32)
            WA = S(st, "WA", [128, 512], BF16)
            guf = S(st, "guf", [128, 512], F32)
            GUP = S(st, "GUP", [128, 512], BF16)
            lngt = S(st, "lngt", [128, 512], F32)
            lnbt = S(st, "lnbt", [128, 512], F32)
            Dm([], [mk], mk.t[:], c_masks)
            Dm([], [cmask], cmask.t[:], c_cmask)
            Dm([], [blk], blk.t[:], c_blk)
            Dm([], [blk2], blk2.t[:], c_blk2)
            Dm([], [hp_], hp_.t[:], hpp[l])
            Dm([], [waf], waf.t[:], wa_up[l])
            Dm([], [guf], guf.t[:], g_up[l])
            Dm([], [lngt], lngt.t[:], lng[l])
            Dm([], [lnbt], lnbt.t[:], lnb[l])
            I("dve", "tensor_copy", [mk], [mkb], out=mkb.t[:], in_=mk.t[:])
            I("dve", "tensor_copy", [waf], [WA], out=WA.t[:], in_=waf.t[:])
            I("dve", "tensor_copy", [guf], [GUP], out=GUP.t[:], in_=guf.t[:])
            I("dve", "tensor_scalar", [hp_], [hp_], out=hp_.t[:, 5, :], in0=hp_.t[:, 3, :], scalar1=-1.0, scalar2=1.0, op0=ALU.mult, op1=ALU.add)
            par = lambda i, h: hp_.t[:, i, h:h + 1]
            su = mkb.t[:, 0, :]; sl = mkb.t[:, 1, :]; iu = mkb.t[:, 2, :]; idr = mkb.t[:, 3, :]

            hs = [S(st, "hs%d" % i, [128, 512], F32) for i in range(14)]
            th12 = S(st, "th12", [128, 512], BF16)
            sg13 = S(st, "sg13", [128, 512], BF16)
            RH = R(st, "RH", [128, 512], BF16, 4); KH = R(st, "KH", [128, 512], BF16, 4)
            BH = R(st, "BH", [128, 512], BF16, 4); AH = R(st, "AH", [128, 512], BF16, 4)
            KT = R(st, "KT", [128, 512], BF16, 4); BT = R(st, "BT", [128, 512], BF16, 4)
            VB = R(st, "VB", [128, 512], BF16, 4)
            gC = R(st, "gC", [128, 8], F32, 4)
            ncs = R(st, "ncs", [128, 8], F32, 2)
            bsum = S(st, "bsum", [64, 8, 8], F32)
            f = {n: R(st, n, [128, 512], F32, 2 if n == "tq" else 1) for n in
                 ("sw", "av", "cs", "ep", "en", "g1", "ed", "kkr", "tq", "kkn", "kmod", "bb")}
            Nb = R(st, "Nb", [64, 512], BF16, 3); Lb = R(st, "Lb", [64, 512], BF16, 3)
            Xb = R(st, "Xb", [64, 512], BF16, 2)
            Lak = R(st, "Lak", [64, 512], BF16, 2); Mrb = R(st, "Mrb", [64, 512], BF16, 2); Mrk = R(st, "Mrk", [64, 512], BF16, 2)
            P2 = R(st, "P2", [64, 512], BF16, 2)
            AtT = R(st, "AtT", [128, 4, 64], BF16, 2)
            tokE = {n: R(st, n + "e", [64, 4, 128], BF16, 2) for n in ("A", "B", "K")}
            tokO = {n: R(st, n + "o", [64, 4, 128], BF16, 2) for n in ("A", "B", "K")}
            Vtok = R(st, "Vtok", [64, 512], BF16, 2)
            Usb = R(st, "Usb", [64, 512], BF16, 2)
            U32 = R(st, "U32", [64, 512], F32, 3)
            H32 = S(st, "H32", [128, 4, 64], F32)
            Hd = R(st, "Hd", [128, 256], F32, 2)
            Hbf = R(st, "Hbf", [128, 4, 64], BF16, 2)
            ysb = R(st, "ysb", [64, 8, 64], F32, 2)
            ysq = R(st, "ysq", [64, 8, 64], F32, 2)
            s8 = R(st, "s8", [64, 8], F32, 8)
            y2 = R(st, "y2", [64, 8, 64], F32, 2)
            yo = R(st, "yo", [64, 512], BF16, 2)
            if os.environ.get("ZPAD", "0") == "1":
                for i_, t_ in enumerate(self.padded):
                    if t_.full.dtype == I32:
                        continue
                    I("pool" if i_ % 2 else "dve", "memset", [], [t_], ap=t_.full[:], constant=0.0)
            for d_ in (tokE, tokO):
                for n in d_:
                    for t_ in d_[n]:
                        I("pool", "memset", [], [t_], ap=t_.t[:], constant=0.0)
            I("dve", "memset", [], [H32], ap=H32.t[:], constant=0.0)
            hb0 = nxt(Hbf)
            I("dve", "memset", [], [hb0], ap=hb0.t[:], constant=0.0)
            hstate = [hb0]

            for ti in range(NT):
                tsl = slice(ti * 512, (ti + 1) * 512)
                for c_ in range(14):
                    Dm([], [hs[c_]], hs[c_].t[:], hrw[c_, :, tsl])
                I("act", "activation", [hs[12]], [th12], out=th12.t[0:64, :], in_=hs[12].t[0:64, :], func=AF.Tanh)
                I("act", "activation", [hs[12]], [th12], out=th12.t[64:128, :], in_=hs[12].t[64:128, :], func=AF.Copy)
                I("act", "activation", [hs[13]], [sg13], out=sg13.t[:], in_=hs[13].t[:], func=AF.Sigmoid)
                for hp in range(4):
                    r_, k_, v_ = hs[hp], hs[4 + hp], hs[8 + hp]
                    hsl = slice(hp * 128, (hp + 1) * 128)
                    psw = self.ps(); psa = self.ps()
                    I("pe", "matmul", [WA, th12], [psw], out=psw.t[:], lhsT=WA.t[0:64, hsl], rhs=th12.t[0:64, :], start=True, stop=True)
                    I("pe", "matmul", [WA, th12], [psa], out=psa.t[:], lhsT=WA.t[64:128, hsl], rhs=th12.t[64:128, :], start=True, stop=True)
                    sw = nxt(f["sw"]); av = nxt(f["av"]); cs = nxt(f["cs"]); ep = nxt(f["ep"]); en = nxt(f["en"])
                    g1 = nxt(f["g1"]); ed = nxt(f["ed"]); kkr = nxt(f["kkr"]); tq = nxt(f["tq"]); kkn = nxt(f["kkn"])
                    kmod = nxt(f["kmod"]); bb = nxt(f["bb"])
                    I("act", "activation", [psw], [sw], out=sw.t[:], in_=psw.t[:], func=AF.Copy)
                    I("act", "activation", [psa], [av], out=av.t[:], in_=psa.t[:], func=AF.Copy)
                    I("act", "activation", [sw, hp_], [sw], out=sw.t[:], in_=sw.t[:], func=AF.Sigmoid, bias=par(0, hp))
                    I("act", "activation", [av, hp_], [av], out=av.t[:], in_=av.t[:], func=AF.Sigmoid, bias=par(1, hp))
                    I("dve", "tensor_tensor_scan", [cmask, sw], [cs], out=cs.t[:], data0=cmask.t[:], data1=sw.t[:], initial=0.0,
                      op0=ALU.mult, op1=ALU.add)
                    I("act", "activation", [cs], [ep], out=ep.t[:], in_=cs.t[:], func=AF.Exp, scale=-c)
                    I("act", "activation", [cs], [en], out=en.t[:], in_=cs.t[:], func=AF.Exp, scale=c)
                    I("pool", "tensor_tensor", [cs, sw], [tq], out=tq.t[:], in0=cs.t[:], in1=sw.t[:], op=ALU.subtract)
                    I("act", "activation", [tq], [g1], out=g1.t[:], in_=tq.t[:], func=AF.Exp, scale=-c)
                    nc_ = nxt(ncs)
                    I("dve", "tensor_scalar", [cs], [nc_], out=nc_.t[:], in0=cs.t[:].rearrange("p (a b) -> p a b", b=64)[:, :, 63],
                      scalar1=-c, scalar2=None, op0=ALU.mult)
                    for ch in range(8):
                        I("act", "activation", [cs, nc_], [ed], out=ed.t[:, ch * 64:(ch + 1) * 64], in_=cs.t[:, ch * 64:(ch + 1) * 64],
                          func=AF.Exp, scale=c, bias=nc_.t[:, ch:ch + 1])
                    gc = gC[hp]
                    I("act", "activation", [nc_], [gc], out=gc.t[:], in_=nc_.t[:], func=AF.Exp)
                    I("dve", "tensor_scalar", [k_, hp_], [kkr], out=kkr.t[:], in0=k_.t[:], scalar1=par(2, hp), scalar2=None, op0=ALU.mult)
                    I("pool", "tensor_tensor", [kkr], [tq], out=tq.t[:], in0=kkr.t[:], in1=kkr.t[:], op=ALU.mult)
                    pss = self.ps()
                    I("pe", "matmul", [blk, tq], [pss], out=pss.t[:], lhsT=blk.t[:], rhs=tq.t[:], start=True, stop=True)
                    I("act", "activation", [pss], [tq], out=tq.t[:], in_=pss.t[:], func=AF.Sqrt, bias=1e-12)
                    I("dve", "reciprocal", [tq], [tq], out=tq.t[:], in_=tq.t[:])
                    I("dve", "tensor_tensor", [kkr, tq], [kkn], out=kkn.t[:], in0=kkr.t[:], in1=tq.t[:], op=ALU.mult)
                    I("dve", "tensor_scalar", [av, hp_], [tq], out=tq.t[:], in0=av.t[:], scalar1=par(3, hp), scalar2=par(5, hp), op0=ALU.mult, op1=ALU.add)
                    I("pool", "tensor_tensor", [k_, tq], [kmod], out=kmod.t[:], in0=k_.t[:], in1=tq.t[:], op=ALU.mult)
                    I("pool", "tensor_tensor", [kkn, av], [bb], out=bb.t[:], in0=kkn.t[:], in1=av.t[:], op=ALU.mult)
                    I("dve", "tensor_tensor", [r_, ep], [RH[hp]], out=RH[hp].t[:], in0=r_.t[:], in1=ep.t[:], op=ALU.mult)
                    I("pool", "tensor_tensor", [kmod, en], [KH[hp]], out=KH[hp].t[:], in0=kmod.t[:], in1=en.t[:], op=ALU.mult)
                    I("dve", "tensor_tensor", [bb, en], [BH[hp]], out=BH[hp].t[:], in0=bb.t[:], in1=en.t[:], op=ALU.mult)
                    I("dve", "scalar_tensor_tensor", [kkn, g1], [AH[hp]], out=AH[hp].t[:], in0=kkn.t[:], scalar=-1.0, in1=g1.t[:], op0=ALU.mult, op1=ALU.mult)
                    I("pool", "tensor_tensor", [kmod, ed], [KT[hp]], out=KT[hp].t[:], in0=kmod.t[:], in1=ed.t[:], op=ALU.mult)
                    I("dve", "tensor_tensor", [bb, ed], [BT[hp]], out=BT[hp].t[:], in0=bb.t[:], in1=ed.t[:], op=ALU.mult)
                    I("act", "activation", [v_], [VB[hp]], out=VB[hp].t[:], in_=v_.t[:], func=AF.Copy)
                    if ti == 0 and hp == 0:
                        for nm, tl in (("sw", sw), ("av", av), ("cs", cs), ("ep", ep), ("en", en), ("g1", g1), ("ed", ed), ("kkn", kkn),
                                       ("kmod", kmod), ("bb", bb), ("RH", RH[0]), ("KH", KH[0]), ("BH", BH[0]), ("AH", AH[0]),
                                       ("KT", KT[0]), ("BT", BT[0]), ("VB", VB[0]), ("gC", gc)):
                            self.dump(nm, tl, tl.t[:])
                    I("pool", "tensor_tensor", [r_, kmod], [tq], out=tq.t[:], in0=r_.t[:], in1=kmod.t[:], op=ALU.mult)
                    I("dve", "tensor_scalar", [tq, hp_], [tq], out=tq.t[:], in0=tq.t[:], scalar1=par(4, hp), scalar2=None, op0=ALU.mult)
                    psb_ = self.ps()
                    for ch in range(8):
                        I("pe", "matmul", [tq, blk2], [psb_], out=psb_.t[0:64, ch * 2:ch * 2 + 2], lhsT=tq.t[:, ch * 64:(ch + 1) * 64],
                          rhs=blk2.t[:], start=True, stop=True)
                    I("dve", "tensor_copy", [psb_], [bsum], out=bsum.t[:, :, 2 * hp:2 * hp + 2],
                      in_=psb_.t[0:64, 0:16].rearrange("p (a b) -> p a b", b=2))

                PF = self.psf
                PB = self.psb

                def v4(ap):
                    return ap.rearrange("p (a b c) -> p a b c", b=2, c=64)

                def v3(ap):
                    return ap.rearrange("p (a c) -> p a c", c=64)

                def pre(ch, C):
                    csl = slice(ch * 64, (ch + 1) * 64)

                    def scores(lh, rh, mask, dst):
                        pp = (PF[0], PF[1])
                        for h in range(8):
                            hp, pb = h // 2, (h % 2) * 64
                            ps = pp[h % 2]
                            I("pe", "matmul", [lh[hp], rh[hp]], [ps], out=ps.t[0:64, hp * 64:(hp + 1) * 64], lhsT=lh[hp].t[pb:pb + 64, csl],
                              rhs=rh[hp].t[pb:pb + 64, csl], start=True, stop=True)
                        for par_ in range(2):
                            I("dve", "tensor_tensor", [pp[par_], mkb], [dst], out=v4(dst.t[:])[:, :, par_, :], in0=v3(pp[par_].t[0:64, 0:256]),
                              in1=v4(mask)[:, :, par_, :], op=ALU.mult)
                    n0 = nxt(Nb); l0 = nxt(Lb); lak = nxt(Lak); mrb = nxt(Mrb); mrk = nxt(Mrk)
                    scores(BH, AH, su, n0)
                    yield
                    scores(AH, BH, sl, l0)
                    yield
                    scores(KH, AH, su, lak)
                    yield
                    scores(BH, RH, iu, mrb)
                    yield
                    scores(KH, RH, iu, mrk)
                    yield
                    X = nxt(Xb)
                    I("dve", "tensor_tensor", [n0, mkb], [X], out=X.t[:], in0=n0.t[:], in1=idr, op=ALU.add)
                    nk, lk = n0, l0
                    for kk_ in range(1, 6):
                        l_new = nxt(Lb)
                        psl = PF[0]
                        for h in range(8):
                            hs_ = slice(h * 64, (h + 1) * 64)
                            I("pe", "matmul", [nk, lk], [psl], out=psl.t[0:64, hs_], lhsT=nk.t[:, hs_], rhs=lk.t[:, hs_], start=True, stop=True)
                        if kk_ < 5:
                            n_new = nxt(Nb)
                            psn = PF[1]
                            for h in range(8):
                                hs_ = slice(h * 64, (h + 1) * 64)
                                I("pe", "matmul", [nk, lk], [psn], out=psn.t[0:64, hs_], lhsT=lk.t[:, hs_], rhs=nk.t[:, hs_], start=True, stop=True)
                        yield
                        I("act", "activation", [psl], [l_new], out=l_new.t[:], in_=psl.t[0:64, :], func=AF.Copy)
                        if kk_ < 5:
                            I("dve", "tensor_copy", [psn], [n_new], out=n_new.t[:], in_=psn.t[0:64, :])
                        yield
                        psx = PF[2]
                        for h in range(8):
                            hs_ = slice(h * 64, (h + 1) * 64)
                            I("pe", "matmul", [l_new, X], [psx], out=psx.t[0:64, hs_], lhsT=l_new.t[:, hs_], rhs=X.t[:, hs_], start=True, stop=True)
                        yield
                        Xn = nxt(Xb)
                        I("dve", "tensor_tensor", [psx, X], [Xn], out=Xn.t[:], in0=psx.t[0:64, :], in1=X.t[:], op=ALU.add)
                        X = Xn
                        lk = l_new
                        if kk_ < 5:
                            nk = n_new
                        yield
                    te = {n: nxt(tokE[n]) for n in tokE}
                    to = {n: nxt(tokO[n]) for n in tokO}
                    vt = nxt(Vtok)
                    for i_, (n, srcl) in enumerate((("A", AH), ("B", BT), ("K", KT))):
                        pt = PB[i_ % 2]
                        for hp in range(4):
                            I("pe", "transpose", [srcl[hp], self.ident], [pt], out=pt.t[0:64, hp * 128:(hp + 1) * 128], in_=srcl[hp].t[:, csl],
                              identity=self.ident.t[:, :])
                        yield
                        pv = pt.t[0:64, 0:512].rearrange("p (a b) -> p a b", b=128)
                        I("dve", "tensor_copy", [pt], [te[n]], out=te[n].t[:, :, 0:64], in_=pv[:, :, 0:64])
                        I("dve", "tensor_copy", [pt], [to[n]], out=to[n].t[:, :, 64:128], in_=pv[:, :, 64:128])
                        yield
                    pt = PB[1]
                    for hp in range(4):
                        I("pe", "transpose", [VB[hp], self.ident], [pt], out=pt.t[0:64, hp * 128:(hp + 1) * 128], in_=VB[hp].t[:, csl],
                          identity=self.ident.t[:, :])
                    yield
                    I("dve", "tensor_copy", [pt], [vt], out=vt.t[:], in_=pt.t[0:64, 0:512])
                    att = nxt(AtT)
                    psA = PF[0]
                    for hp in range(4):
                        I("pe", "matmul", [te["A"], X], [psA], out=psA.t[:, hp * 64:(hp + 1) * 64], lhsT=te["A"].t[:, hp, :],
                          rhs=X.t[:, (2 * hp) * 64:(2 * hp + 1) * 64], start=True, stop=False)
                        I("pe", "matmul", [to["A"], X], [psA], out=psA.t[:, hp * 64:(hp + 1) * 64], lhsT=to["A"].t[:, hp, :],
                          rhs=X.t[:, (2 * hp + 1) * 64:(2 * hp + 2) * 64], start=False, stop=True)
                    yield
                    I("act", "activation", [psA], [att], out=att.t[:], in_=psA.t[:, 0:256].rearrange("p (a b) -> p a b", b=64), func=AF.Copy)
                    p2 = nxt(P2)
                    psP = PF[1]
                    for h in range(8):
                        hs_ = slice(h * 64, (h + 1) * 64)
                        I("pe", "matmul", [lak, vt], [psP], out=psP.t[0:64, hs_], lhsT=lak.t[:, hs_], rhs=vt.t[:, hs_], start=True, stop=True)
                    yield
                    I("act", "activation", [psP], [p2], out=p2.t[:], in_=psP.t[0:64, :], func=AF.Copy)
                    w2 = nxt(U32)
                    psW = PF[2]
                    for h in range(8):
                        hs_ = slice(h * 64, (h + 1) * 64)
                        I("pe", "matmul", [X, p2], [psW], out=psW.t[0:64, hs_], lhsT=X.t[:, hs_], rhs=p2.t[:, hs_], start=True, stop=True)
                    yield
                    I("act", "activation", [psW], [w2], out=w2.t[:], in_=psW.t[0:64, :], func=AF.Copy)
                    C.update(X=X, mrb=mrb, mrk=mrk, te=te, to=to, vt=vt, att=att, w2=w2)
                    yield

                def seq(ch, C, hstate):
                    csl = slice(ch * 64, (ch + 1) * 64)
                    tok0 = ti * 512 + ch * 64
                    mrb, mrk, te, to, vt, att, w2 = C["mrb"], C["mrk"], C["te"], C["to"], C["vt"], C["att"], C["w2"]
                    hcur = hstate[0]
                    psUp = (PF[4], PF[5])
                    for h in range(8):
                        hp, pb = h // 2, (h % 2) * 64
                        pq = psUp[h % 2]
                        I("pe", "matmul", [att, hcur], [pq], out=pq.t[0:64, hp * 64:(hp + 1) * 64], lhsT=att.t[pb:pb + 64, hp, :],
                          rhs=hcur.t[pb:pb + 64, hp, :], start=True, stop=True)
                    yield
                    us = nxt(Usb)
                    for par_ in range(2):
                        I("dve", "tensor_tensor", [psUp[par_], w2], [us], out=v4(us.t[:])[:, :, par_, :], in0=v3(psUp[par_].t[0:64, 0:256]),
                          in1=v4(w2.t[:])[:, :, par_, :], op=ALU.add)
                    yield
                    psH = PF[3]
                    for hp in range(4):
                        o_ = psH.t[:, hp * 64:(hp + 1) * 64]
                        he, ho_ = slice(2 * hp * 64, (2 * hp + 1) * 64), slice((2 * hp + 1) * 64, (2 * hp + 2) * 64)
                        I("pe", "matmul", [te["B"], us], [psH], out=o_, lhsT=te["B"].t[:, hp, :], rhs=us.t[:, he], start=True, stop=False)
                        I("pe", "matmul", [to["B"], us], [psH], out=o_, lhsT=to["B"].t[:, hp, :], rhs=us.t[:, ho_], start=False, stop=False)
                        I("pe", "matmul", [te["K"], vt], [psH], out=o_, lhsT=te["K"].t[:, hp, :], rhs=vt.t[:, he], start=False, stop=False)
                        I("pe", "matmul", [to["K"], vt], [psH], out=o_, lhsT=to["K"].t[:, hp, :], rhs=vt.t[:, ho_], start=False, stop=True)
                    yield
                    psYp = (PF[4], PF[5])
                    for h in range(8):
                        hp, pb = h // 2, (h % 2) * 64
                        pq = psYp[h % 2]
                        I("pe", "matmul", [RH[hp], hcur], [pq], out=pq.t[0:64, hp * 64:(hp + 1) * 64], lhsT=RH[hp].t[pb:pb + 64, csl],
                          rhs=hcur.t[pb:pb + 64, hp, :], start=True, stop=True)
                    yield
                    hd_ = nxt(Hd)
                    I("act", "activation", [psH], [hd_], out=hd_.t[:], in_=psH.t[:, 0:256], func=AF.Copy)
                    yield
                    for hp in range(4):
                        I("dve", "scalar_tensor_tensor", [H32, gC[hp], hd_], [H32], out=H32.t[:, hp, :], in0=H32.t[:, hp, :],
                          scalar=gC[hp].t[:, ch:ch + 1], in1=hd_.t[:, hp * 64:(hp + 1) * 64], op0=ALU.mult, op1=ALU.add)
                    hnew = nxt(Hbf)
                    I("act", "activation", [H32], [hnew], out=hnew.t[:], in_=H32.t[:], func=AF.Copy)
                    hstate[0] = hnew
                    yield
                    y = nxt(ysb)
                    yflat = y.t[:].rearrange("p a b -> p (a b)")
                    y0 = nxt(U32)
                    for par_ in range(2):
                        I("dve", "tensor_copy", [psYp[par_]], [y0], out=v4(y0.t[:])[:, :, par_, :], in_=v3(psYp[par_].t[0:64, 0:256]))
                    yield
                    psY = PF[3]
                    for h in range(8):
                        hs_ = slice(h * 64, (h + 1) * 64)
                        I("pe", "matmul", [mrb, us], [psY], out=psY.t[0:64, hs_], lhsT=mrb.t[:, hs_], rhs=us.t[:, hs_], start=True, stop=False)
                        I("pe", "matmul", [mrk, vt], [psY], out=psY.t[0:64, hs_], lhsT=mrk.t[:, hs_], rhs=vt.t[:, hs_], start=False, stop=True)
                    psG = PF[4]
                    I("pe", "matmul", [sg13, GUP], [psG], out=psG.t[0:64, :], lhsT=sg13.t[:, csl], rhs=GUP.t[:], start=True, stop=True)
                    yield
                    I("dve", "tensor_tensor", [psY, y0], [y], out=yflat, in0=psY.t[0:64, :], in1=y0.t[:], op=ALU.add)
                    yield
                    yq = nxt(ysq)
                    I("pool", "tensor_tensor", [y], [yq], out=yq.t[:], in0=y.t[:], in1=y.t[:], op=ALU.mult)
                    m1 = nxt(s8); m2 = nxt(s8); vr = nxt(s8)
                    I("dve", "tensor_reduce", [y], [m1], out=m1.t[:], in_=y.t[:], axis=AX.X, op=ALU.add)
                    yield
                    I("dve", "tensor_reduce", [yq], [m2], out=m2.t[:], in_=yq.t[:], axis=AX.X, op=ALU.add)
                    I("dve", "tensor_scalar", [m1], [m1], out=m1.t[:], in0=m1.t[:], scalar1=1.0 / 64, scalar2=None, op0=ALU.mult)
                    yield
                    I("dve", "tensor_tensor", [m1], [vr], out=vr.t[:], in0=m1.t[:], in1=m1.t[:], op=ALU.mult)
                    I("dve", "scalar_tensor_tensor", [m2, vr], [vr], out=vr.t[:], in0=m2.t[:], scalar=1.0 / 64, in1=vr.t[:], op0=ALU.mult, op1=ALU.subtract)
                    yield
                    I("act", "activation", [vr], [vr], out=vr.t[:], in_=vr.t[:], func=AF.Sqrt, bias=64e-5)
                    yield
                    I("dve", "reciprocal", [vr], [vr], out=vr.t[:], in_=vr.t[:])
                    yy = nxt(y2)
                    I("dve", "tensor_tensor", [y, m1], [yy], out=yy.t[:], in0=y.t[:], in1=m1.t[:].unsqueeze(2).broadcast_to([64, 8, 64]), op=ALU.subtract)
                    yield
                    I("dve", "tensor_tensor", [yy, vr], [yy], out=yy.t[:], in0=yy.t[:], in1=vr.t[:].unsqueeze(2).broadcast_to([64, 8, 64]), op=ALU.mult)
                    yyf = yy.t[:].rearrange("p a b -> p (a b)")
                    yield
                    I("pool", "tensor_tensor", [yy, lngt], [yy], out=yyf, in0=yyf, in1=lngt.t[0:64, :], op=ALU.mult)
                    I("dve", "tensor_tensor", [vt, bsum], [yq], out=yq.t[:], in0=vt.t[:].rearrange("p (a b) -> p a b", b=64),
                      in1=bsum.t[:, ch, :].unsqueeze(2).broadcast_to([64, 8, 64]), op=ALU.mult)
                    yield
                    I("pool", "tensor_tensor", [yy, lnbt], [yy], out=yyf, in0=yyf, in1=lnbt.t[0:64, :], op=ALU.add)
                    yield
                    I("pool", "tensor_tensor", [yy, yq], [yy], out=yy.t[:], in0=yy.t[:], in1=yq.t[:], op=ALU.add)
                    yield
                    yo_ = nxt(yo)
                    I("dve", "tensor_tensor", [psG, yy], [yo_], out=yo_.t[:], in0=psG.t[0:64, :], in1=yyf, op=ALU.mult)
                    Dm([yo_], [], mixed[tok0:tok0 + 64, 0:512], yo_.t[:])
                    yield

                Cs = [dict() for _ in range(8)]
                run_interleaved([pre(0, Cs[0])])
                for ch in range(8):
                    gl = [seq(ch, Cs[ch], hstate)]
                    if ch + 1 < 8:
                        gl.append(pre(ch + 1, Cs[ch + 1]))
                    run_interleaved(gl)


    def phase_sb(self, l, qT_d, kT_d, v_d, mixed, col0, c_tri, c_diag):
        I, Dm, S, R, nxt, p = self.I, self.Dm, self.S, self.R, self.nxt, self.p
        T, NB = self.T, self.NB
        NQ = T // 512
        with contextlib.ExitStack() as st:
            trif = S(st, "trif", [128, 128], F32)
            tri = S(st, "tri", [128, 128], BF16)
            dmf = S(st, "dmf", [128, 128], F32)
            dmask = S(st, "dmask", [128, 128], BF16)
            ones1 = S(st, "ones1", [128, 1], BF16)
            Dm([], [trif], trif.t[:], c_tri)
            Dm([], [dmf], dmf.t[:], c_diag)
            I("dve", "tensor_copy", [trif], [tri], out=tri.t[:], in_=trif.t[:])
            I("dve", "tensor_copy", [dmf], [dmask], out=dmask.t[:], in_=dmf.t[:])
            I("dve", "memset", [], [ones1], ap=ones1.t[:], constant=1.0)

            def head(h):
                kt = S(st, "sKT", [64, T], BF16)
                vt = S(st, "sV", [128, NB, 64], BF16)
                QT = R(st, "sQT", [64, 512], BF16, 2)
                ee = R(st, "see", [128, 512], F32, 2)
                spb = R(st, "sspb", [128, 512], BF16, 2)
                ecb = R(st, "secb", [128, 512], F32, 2)
                Wb = R(st, "sWb", [128, 512], BF16, 2)
                acc = R(st, "sacc", [128, 4, 64], F32, 2)
                carry = R(st, "scarry", [128, 4], F32, 2)
                ecar = R(st, "secar", [128, 4], F32, 2)
                posb = R(st, "sposb", [128, 256], F32, 2)
                ob = R(st, "sob", [128, 4, 64], BF16, 2)
                zc = self.psf[h]
                if h < 2:
                    po = self.psf[4 + h]
                else:
                    po = self.psb_f32[h - 2]
                Dm([], [kt], kt.t[:, :], kT_d[h * 64:(h + 1) * 64, :])
                Dm([], [vt], vt.t[:, :, :], v_d.rearrange("(n p) c -> p n c", p=128)[:, :, h * 64:(h + 1) * 64])
                yield
                for qi in range(NQ):
                    q = nxt(QT)
                    Dm([], [q], q.t[:, :], qT_d[h * 64:(h + 1) * 64, qi * 512:(qi + 1) * 512])
                    nkb = (qi + 1) * 4
                    a_ = nxt(acc); cy = nxt(carry)
                    I("dve", "memset", [], [a_], ap=a_.t[:], constant=0.0)
                    I("dve", "memset", [], [cy], ap=cy.t[:], constant=0.0)
                    yield
                    for kb in range(nkb - 1, -1, -1):
                        s0 = max(0, kb - qi * 4)
                        qs = slice(s0 * 128, 512)
                        diag = kb >= qi * 4
                        dsl = slice(s0 * 128, (s0 + 1) * 128)
                        I("pe", "matmul", [kt, q], [zc], out=zc.t[:, qs], lhsT=kt.t[:, kb * 128:(kb + 1) * 128], rhs=q.t[:, qs], start=True, stop=True)
                        yield
                        e = nxt(ee); sp = nxt(spb); ec = nxt(ecb); W = nxt(Wb)
                        I("act", "activation", [zc], [e], out=e.t[:, qs], in_=zc.t[:, qs], func=AF.Exp)
                        yield
                        I("act", "activation", [e], [sp], out=sp.t[:, qs], in_=e.t[:, qs], func=AF.Ln, bias=1.0)
                        if diag:
                            I("pool", "tensor_tensor", [sp, dmask], [sp], out=sp.t[:, dsl], in0=sp.t[:, dsl], in1=dmask.t[:], op=ALU.mult)
                        yield
                        I("pe", "matmul", [tri, sp], [zc], out=zc.t[:, qs], lhsT=tri.t[:], rhs=sp.t[:, qs], start=True, stop=True)
                        for s in range(s0, 4):
                            I("pe", "matmul", [sp, ones1], [po], out=po.t[:, s:s + 1], lhsT=sp.t[:, s * 128:(s + 1) * 128], rhs=ones1.t[:],
                              start=True, stop=True)
                        yield
                        ecr = nxt(ecar)
                        I("act", "activation", [zc], [ec], out=ec.t[:, qs], in_=zc.t[:, qs], func=AF.Exp, scale=-1.0)
                        I("act", "activation", [cy], [ecr], out=ecr.t[:], in_=cy.t[:], func=AF.Exp, scale=-1.0)
                        yield
                        I("pool", "tensor_tensor", [e, ec], [W], out=W.t[:, qs], in0=e.t[:, qs], in1=ec.t[:, qs], op=ALU.mult)
                        if diag:
                            I("pool", "tensor_tensor", [W, dmask], [W], out=W.t[:, dsl], in0=W.t[:, dsl], in1=dmask.t[:], op=ALU.mult)
                        I("dve", "tensor_tensor", [po, cy], [cy], out=cy.t[:, s0:4], in0=po.t[:, s0:4], in1=cy.t[:, s0:4], op=ALU.add)
                        yield
                        for s in range(s0, 4):
                            I("pe", "matmul", [W, vt], [po], out=po.t[:, s * 64:(s + 1) * 64], lhsT=W.t[:, s * 128:(s + 1) * 128], rhs=vt.t[:, kb, :],
                              start=True, stop=True)
                        yield
                        pos_ = nxt(posb)
                        I("dve", "tensor_copy", [po], [pos_], out=pos_.t[:, s0 * 64:256], in_=po.t[:, s0 * 64:256])
                        for s in range(s0, 4):
                            I("dve", "scalar_tensor_tensor", [pos_, ecr, a_], [a_], out=a_.t[:, s, :], in0=pos_.t[:, s * 64:(s + 1) * 64],
                              scalar=ecr.t[:, s:s + 1], in1=a_.t[:, s, :], op0=ALU.mult, op1=ALU.add)
                        yield
                    o_ = nxt(ob)
                    I("act", "activation", [a_], [o_], out=o_.t[:], in_=a_.t[:], func=AF.Copy)
                    Dm([o_], [], mixed[qi * 512:(qi + 1) * 512, col0 + h * 64:col0 + (h + 1) * 64].rearrange("(s p) c -> p s c", p=128), o_.t[:])
                    yield

            run_interleaved([head(h) for h in range(4)])

    def phase_attn_gen(self, l, kind, qT_d, kT_d, v_d, mixed, col0, c_tri, c_diag):
        I, Dm, S, R, nxt, p = self.I, self.Dm, self.S, self.R, self.nxt, self.p
        T, NB = self.T, self.NB
        dk = 64 if kind == "sb" else 96
        NQ = T // 512
        with contextlib.ExitStack() as st:
            trif = S(st, "trif", [128, 128], F32)
            tri = S(st, "tri", [128, 128], BF16)
            dmf = S(st, "dmf", [128, 128], F32)
            dmask = S(st, "dmask", [128, 128], BF16)
            ones1 = S(st, "ones1", [128, 1], BF16)
            Dm([], [trif], trif.t[:], c_tri)
            Dm([], [dmf], dmf.t[:], c_diag)
            I("dve", "tensor_copy", [trif], [tri], out=tri.t[:], in_=trif.t[:])
            I("dve", "tensor_copy", [dmf], [dmask], out=dmask.t[:], in_=dmf.t[:])
            I("dve", "memset", [], [ones1], ap=ones1.t[:], constant=1.0)
            KTt = R(st, "KTt", [dk, T], BF16, 2)
            Vt = R(st, "Vt", [128, NB, 65], BF16, 2)
            QT = R(st, "QT", [dk, 512], BF16, 2)
            ee = R(st, "ee", [128, 512], F32, 3)
            spb = R(st, "spb", [128, 512], BF16, 3)
            ecb = R(st, "ecb", [128, 512], F32, 3)
            Wb = R(st, "Wb", [128, 512], BF16, 3)
            acc = R(st, "acc", [128, 4, 64], F32, 2)
            carry = R(st, "carry", [128, 4], F32, 2)
            ecar = R(st, "ecar", [128, 4], F32, 4)
            rden = R(st, "rden", [128, 4], F32, 2)
            posb = R(st, "posb", [128, 256], F32, 2)
            mlaW = [S(st, "mlaW%d" % kb, [128, 512], BF16) for kb in range(NB)] if kind == "mla" else None
            ob = R(st, "ob", [128, 4, 64], BF16, 2)
            for h in range(4):
                kt = nxt(KTt)
                vt = nxt(Vt)
                Dm([], [kt], kt.t[:, :], kT_d[h * dk:(h + 1) * dk, :])
                I("pool", "memset", [], [vt], ap=vt.t[:, :, 64:65], constant=1.0)
                Dm([], [vt], vt.t[:, :, 0:64], v_d.rearrange("(n p) c -> p n c", p=128)[:, :, h * 64:(h + 1) * 64])
                for qi in range(NQ):
                    q = nxt(QT)
                    Dm([], [q], q.t[:, :], qT_d[h * dk:(h + 1) * dk, qi * 512:(qi + 1) * 512])
                    nkb = (qi + 1) * 4
                    if kind == "sb":
                        a_ = nxt(acc); cy = nxt(carry)
                        I("dve", "memset", [], [a_], ap=a_.t[:], constant=0.0)
                        I("dve", "memset", [], [cy], ap=cy.t[:], constant=0.0)
                        for kb in range(nkb - 1, -1, -1):
                            s0 = max(0, kb - qi * 4)
                            qs = slice(s0 * 128, 512)
                            nq = 512 - s0 * 128
                            psz = self.ps()
                            I("pe", "matmul", [kt, q], [psz], out=psz.t[:, qs], lhsT=kt.t[:, kb * 128:(kb + 1) * 128], rhs=q.t[:, qs], start=True, stop=True)
                            e = nxt(ee); sp = nxt(spb); ec = nxt(ecb); W = nxt(Wb)
                            I("act", "activation", [psz], [e], out=e.t[:, qs], in_=psz.t[:, qs], func=AF.Exp)
                            I("act", "activation", [e], [sp], out=sp.t[:, qs], in_=e.t[:, qs], func=AF.Ln, bias=1.0)
                            diag = (kb - qi * 4 == s0) and (kb >= qi * 4)
                            if diag:
                                dsl = slice(s0 * 128, (s0 + 1) * 128)
                                I("pool", "tensor_tensor", [sp, dmask], [sp], out=sp.t[:, dsl], in0=sp.t[:, dsl], in1=dmask.t[:], op=ALU.mult)
                            psc = self.ps()
                            I("pe", "matmul", [tri, sp], [psc], out=psc.t[:, qs], lhsT=tri.t[:], rhs=sp.t[:, qs], start=True, stop=True)
                            I("act", "activation", [psc], [ec], out=ec.t[:, qs], in_=psc.t[:, qs], func=AF.Exp, scale=-1.0)
                            I("dve", "tensor_tensor", [e, ec], [W], out=W.t[:, qs], in0=e.t[:, qs], in1=ec.t[:, qs], op=ALU.mult)
                            if diag:
                                I("pool", "tensor_tensor", [W, dmask], [W], out=W.t[:, dsl], in0=W.t[:, dsl], in1=dmask.t[:], op=ALU.mult)
                            pso = self.ps()
                            ecr = nxt(ecar)
                            I("act", "activation", [cy], [ecr], out=ecr.t[:], in_=cy.t[:], func=AF.Exp, scale=-1.0)
                            for s in range(s0, 4):
                                ss_ = slice(s * 128, (s + 1) * 128)
                                I("pe", "matmul", [W, vt], [pso], out=pso.t[:, s * 64:(s + 1) * 64], lhsT=W.t[:, ss_], rhs=vt.t[:, kb, 0:64], start=True, stop=True)
                            pos_ = nxt(posb)
                            I("act", "activation", [pso], [pos_], out=pos_.t[:, s0 * 64:256], in_=pso.t[:, s0 * 64:256], func=AF.Copy)
                            for s in range(s0, 4):
                                I("dve", "scalar_tensor_tensor", [pos_, ecr, a_], [a_], out=a_.t[:, s, :], in0=pos_.t[:, s * 64:(s + 1) * 64],
                                  scalar=ecr.t[:, s:s + 1], in1=a_.t[:, s, :], op0=ALU.mult, op1=ALU.add)
                            psk = self.ps()
                            for s in range(s0, 4):
                                ss_ = slice(s * 128, (s + 1) * 128)
                                I("pe", "matmul", [sp, ones1], [psk], out=psk.t[:, s:s + 1], lhsT=sp.t[:, ss_], rhs=ones1.t[:], start=True, stop=True)
                            I("dve", "tensor_tensor", [psk, cy], [cy], out=cy.t[:, s0:4], in0=psk.t[:, s0:4], in1=cy.t[:, s0:4], op=ALU.add)
                        o_ = nxt(ob)
                        I("act", "activation", [a_], [o_], out=o_.t[:], in_=a_.t[:], func=AF.Copy)
                    else:
                        pso = self.ps()
                        Ws = {}
                        for kb in range(nkb):
                            s0 = max(0, kb - qi * 4)
                            qs = slice(s0 * 128, 512)
                            psz = self.ps()
                            I("pe", "matmul", [kt, q], [psz], out=psz.t[:, qs], lhsT=kt.t[:, kb * 128:(kb + 1) * 128], rhs=q.t[:, qs], start=True, stop=True)
                            Wt = mlaW[kb]
                            I("act", "activation", [psz], [Wt], out=Wt.t[:, qs], in_=psz.t[:, qs], func=AF.Exp)
                            if kb >= qi * 4:
                                dsl = slice(s0 * 128, (s0 + 1) * 128)
                                I("pool", "tensor_tensor", [Wt, dmask], [Wt], out=Wt.t[:, dsl], in0=Wt.t[:, dsl], in1=dmask.t[:], op=ALU.mult)
                            Ws[kb] = Wt
                            yield
                        for s in range(4):
                            last = qi * 4 + s
                            for kb in range(last + 1):
                                I("pe", "matmul", [Ws[kb], vt], [pso], out=pso.t[:, s * 65:(s + 1) * 65], lhsT=Ws[kb].t[:, s * 128:(s + 1) * 128],
                                  rhs=vt.t[:, kb, :], start=(kb == 0), stop=(kb == last))
                                if kb % 4 == 3:
                                    yield
                        rd = nxt(rden)
                        pv = pso.t[:, 0:260].rearrange("p (a b) -> p a b", b=65)
                        I("dve", "reciprocal", [pso], [rd], out=rd.t[:], in_=pv[:, :, 64])
                        o_ = nxt(ob)
                        I("dve", "tensor_tensor", [pso, rd], [o_], out=o_.t[:], in0=pv[:, :, 0:64], in1=rd.t[:].unsqueeze(2).broadcast_to([128, 4, 64]), op=ALU.mult)
                    Dm([o_], [], mixed[qi * 512:(qi + 1) * 512, col0 + h * 64:col0 + (h + 1) * 64].rearrange("(s p) c -> p s c", p=128), o_.t[:])

    def ffn_conv_setup(self, st2):
        self.cvf = self.R(st2, "cvf", [128, FF], F32, 2)
        self.cvb = self.R(st2, "cvb", [128, FF], BF16, 2)

    def ffn_conv_gen(self, l, wg, wu, wd):
        I, Dm, nxt = self.I, self.Dm, self.nxt
        cf, cb = self.cvf, self.cvb
        ci = 0
        for (src, dst) in ((wg, self.wgs), (wu, self.wus)):
            dv = dst.rearrange("j p (k c) -> p j k c", c=128)
            for k in range(8):
                f_ = nxt(cf); b_ = nxt(cb)
                Dm([], [f_], f_.t[:], src[l, k * 128:(k + 1) * 128, :])
                yield
                I("pool" if ci % 2 else "dve", "tensor_copy", [f_], [b_], out=b_.t[:], in_=f_.t[:])
                yield
                for hh in range(2):
                    Dm([b_], [], dv[:, hh * 11:(hh + 1) * 11, k, :], b_.t[:, hh * 1408:(hh + 1) * 1408].rearrange("p (j c) -> p j c", c=128))
                ci += 1
                yield
        wdv = wd[l].rearrange("(j p) c -> p j c", p=128)
        for g_ in range(NJ // 2):
            f_ = nxt(cf); b_ = nxt(cb)
            Dm([], [f_], f_.t[:, 0:2048].rearrange("p (j c) -> p j c", c=1024), wdv[:, 2 * g_:2 * g_ + 2, :])
            yield
            I("pool" if ci % 2 else "dve", "tensor_copy", [f_], [b_], out=b_.t[:, 0:2048], in_=f_.t[:, 0:2048])
            yield
            Dm([b_], [], self.wds.rearrange("j p c -> p j c")[:, 2 * g_:2 * g_ + 2, :], b_.t[:, 0:2048].rearrange("p (j c) -> p j c", c=1024))
            ci += 1
            yield

    def phase_ffn(self, l, xcur, out, mixed, w_o, gF, wg, wu, wd):
        I, Dm, S, R, nxt, p = self.I, self.Dm, self.S, self.R, self.nxt, self.p
        T, NT = self.T, self.NT
        with contextlib.ExitStack() as st:
            Wo = S(st, "Wo", [128, 8, D], BF16)
            with contextlib.ExitStack() as st2:
                stg = R(st2, "stgo", [128, D], F32, 2)
                for k in range(8):
                    s = nxt(stg)
                    Dm([], [s], s.t[:], w_o[l, k * 128:(k + 1) * 128, :])
                    I("pool" if k % 2 else "dve", "tensor_copy", [s], [Wo], out=Wo.t[:, k, :], in_=s.t[:])
                p.barrier()
            gFt = S(st, "gFt", [128, D], F32)
            Dm([], [gFt], gFt.t[:], gF[l])
            mx = R(st, "mx", [128, D], BF16, 2)
            mxT = R(st, "mxT", [128, 8, 128], BF16, 2)
            xs = R(st, "xs", [128, D], F32, 2)
            xn2 = [[S(st, "xnew%d_%d" % (b_, i), [128, D], F32) for i in range(4)] for b_ in range(2)]
            junk = R(st, "junk", [128, D], BF16, 2)
            ssl = R(st, "ss", [128, 1], F32, 4)
            rsl = R(st, "rs", [128, 1], F32, 4)
            stt = (junk, ssl, rsl)
            xnb = R(st, "xnb", [128, D], BF16, 2)
            xnT2 = R(st, "xnT", [128, 8, 512], BF16, 2)
            aT = S(st, "aT", [128, NJ, 512], BF16)
            wgb = R(st, "wgb", [128, 8, 128], BF16, 3); wub = R(st, "wub", [128, 8, 128], BF16, 3)
            wdb = [S(st, "wdb%d" % j, [128, D], BF16) for j in range(NJ)]
            sg = R(st, "sg", [128, 512], F32, 2)

            def front(ti, xnT, xn):
                for j in range(4):
                    r0 = ti * 512 + j * 128
                    m = nxt(mx)
                    Dm([], [m], m.t[:], mixed[r0:r0 + 128, :])
                    xsj = nxt(xs)
                    Dm([], [xsj], xsj.t[:], xcur[r0:r0 + 128, :])
                    yield
                    mT = nxt(mxT)
                    self.transpose_to(m, [m.t[:, k * 128:(k + 1) * 128] for k in range(8)], mT,
                                      lambda pv, mT=mT: (mT.t[:], pv[:, :].rearrange("p (k c) -> p k c", c=128)), 128)
                    yield
                    for half in range(2):
                        ps = self.ps()
                        for k in range(8):
                            I("pe", "matmul", [mT, Wo], [ps], out=ps.t[:], lhsT=mT.t[:, k, :], rhs=Wo.t[:, k, half * 512:(half + 1) * 512],
                              start=(k == 0), stop=(k == 7))
                        I("dve", "tensor_tensor", [ps, xsj], [xn[j]], out=xn[j].t[:, half * 512:(half + 1) * 512], in0=ps.t[:],
                          in1=xsj.t[:, half * 512:(half + 1) * 512], op=ALU.add)
                        yield
                    xb = nxt(xnb)
                    self.rms_rows(stt, xn[j].t[:], xn[j], D, 1e-6, xb.t[:], xb, gFt.t[:], gFt)
                    yield
                    self.transpose_to(xb, [xb.t[:, k * 128:(k + 1) * 128] for k in range(8)], xnT,
                                      lambda pv, j=j, xnT=xnT: (xnT.t[:, :, j * 128:(j + 1) * 128], pv[:, :].rearrange("p (k c) -> p k c", c=128)), 128)
                    yield

            def main(ti, xnT, xn):
                for jj in range(NJ):
                    gb = nxt(wgb); ub = nxt(wub)
                    Dm([], [gb], gb.t[:].rearrange("p k c -> p (k c)"), self.wgs[jj])
                    Dm([], [ub], ub.t[:].rearrange("p k c -> p (k c)"), self.wus[jj])
                    Dm([], [wdb[jj]], wdb[jj].t[:], self.wds[jj])
                    psg = self.ps(); psu = self.ps()
                    for k in range(8):
                        I("pe", "matmul", [gb, xnT], [psg], out=psg.t[:], lhsT=gb.t[:, k, :], rhs=xnT.t[:, k, :], start=(k == 0), stop=(k == 7))
                    for k in range(8):
                        I("pe", "matmul", [ub, xnT], [psu], out=psu.t[:], lhsT=ub.t[:, k, :], rhs=xnT.t[:, k, :], start=(k == 0), stop=(k == 7))
                    s_ = nxt(sg)
                    I("act", "activation", [psg], [s_], out=s_.t[:], in_=psg.t[:], func=AF.Silu)
                    I("dve", "tensor_tensor", [psu, s_], [aT], out=aT.t[:, jj, :], in0=psu.t[:], in1=s_.t[:], op=ALU.mult)
                    yield
                for j in range(4):
                    r0 = ti * 512 + j * 128
                    for half in range(2):
                        ps = self.ps()
                        for jj in range(NJ):
                            I("pe", "matmul", [aT, wdb[jj]], [ps], out=ps.t[:], lhsT=aT.t[:, jj, j * 128:(j + 1) * 128],
                              rhs=wdb[jj].t[:, half * 512:(half + 1) * 512], start=(jj == 0), stop=(jj == NJ - 1))
                        I("dve", "tensor_tensor", [ps, xn[j]], [xn[j]], out=xn[j].t[:, half * 512:(half + 1) * 512], in0=ps.t[:],
                          in1=xn[j].t[:, half * 512:(half + 1) * 512], op=ALU.add)
                        yield
                    Dm([xn[j]], [], out[r0:r0 + 128, :], xn[j].t[:])

            run_interleaved([front(0, xnT2[0], xn2[0])])
            for ti in range(NT):
                gl = [main(ti, xnT2[ti % 2], xn2[ti % 2])]
                if ti + 1 < NT:
                    gl.append(front(ti + 1, xnT2[(ti + 1) % 2], xn2[(ti + 1) % 2]))
                run_interleaved(gl)


def host_consts():
    c = {}
    c["c_ident"] = np.eye(128, dtype=np.float32)
    su = np.triu(np.ones((64, 64), np.float32), 1)
    iu = np.triu(np.ones((64, 64), np.float32), 0)
    sl = np.tril(np.ones((64, 64), np.float32), -1)
    idn = np.eye(64, dtype=np.float32)
    c["c_masks"] = np.stack([np.tile(m, (1, 8)) for m in (su, sl, iu, idn)], axis=1).astype(np.float32)
    cm = np.ones((128, 512), np.float32)
    cm[:, 0::64] = 0.0
    c["c_cmask"] = cm
    blk = np.zeros((128, 128), np.float32)
    blk[:64, :64] = 1.0
    blk[64:, 64:] = 1.0
    c["c_blk"] = blk
    b2 = np.zeros((128, 2), np.float32)
    b2[:64, 0] = 1.0
    b2[64:, 1] = 1.0
    c["c_blk2"] = b2
    j = np.arange(128)
    c["c_tri"] = (j[:, None] >= j[None, :]).astype(np.float32)
    c["c_sbm"] = (j[:, None] < j[None, :]).astype(np.float32)
    c["c_mlam"] = ((j[:, None] // 64) <= (j[None, :] // 64)).astype(np.float32)
    invf = (10000.0 ** (-np.arange(0, 32, 2, dtype=np.float32) / 32)).astype(np.float32)
    c["c_invf"] = np.tile(invf[None, :], (128, 1)).astype(np.float32)
    return c


def rep(a, n=128):
    return np.ascontiguousarray(np.broadcast_to(a[:, None, :], (a.shape[0], n, a.shape[1]))).astype(np.float32)


def host_layout(inputs, b, T, L):
    f = lambda k: np.asarray(inputs[k], dtype=np.float32)[:L]
    m = {}
    m["x"] = np.ascontiguousarray(np.asarray(inputs["x"], np.float32)[b, :T])
    m["pos"] = np.ascontiguousarray(np.asarray(inputs["positions"])[b, :T].astype(np.int32).reshape(T // 128, 128).T)
    m["w_in"] = f("w_in"); m["w_o"] = f("w_o")
    m["wg"] = f("ffn_w_gate"); m["wu"] = f("ffn_w_up"); m["wd"] = f("ffn_w_down")
    m["gA"] = rep(f("attn_norm_g")); m["gF"] = rep(f("ffn_norm_g"))
    m["mu"] = np.ascontiguousarray(f("rw_shift_mu").reshape(L, 14, 128).transpose(0, 2, 1))
    pp = np.stack([f("rw_w0"), f("rw_a0"), f("rw_k_k"), f("rw_k_a"), f("rw_r_k").reshape(L, 512), f("rw_k_a")], axis=1)
    m["hpp"] = np.ascontiguousarray(pp.reshape(L, 6, 4, 128).transpose(0, 3, 1, 2))
    m["wa_up"] = np.ascontiguousarray(np.concatenate([f("rw_w_up"), f("rw_a_up")], axis=1))
    m["g_up"] = f("rw_g_up")
    m["lng"] = rep(f("rw_ln_g")); m["lnb"] = rep(f("rw_ln_b"))
    m["gcq"] = rep(f("mla_cq_norm_g")); m["gckv"] = rep(f("mla_ckv_norm_g"))
    m["gqn"] = rep(f("mla_q_norm_g")); m["gkn"] = rep(f("mla_k_norm_g"))
    m["w_uq"] = f("mla_w_uq"); m["w_ukv"] = f("mla_w_ukv")
    m.update(host_consts())
    return {k: np.ascontiguousarray(v) for k, v in m.items()}


_CACHE = {}


def kernel(**inputs):
    B, T = np.asarray(inputs["x"]).shape[:2]
    L = np.asarray(inputs["w_in"]).shape[0]
    key = (T, L)
    if key not in _CACHE:
        _CACHE[key] = Builder(T, L).build()
    nc = _CACHE[key]
    n = 8
    in_maps = [host_layout(inputs, c % B, T, L) for c in range(n)]
    res = run_bass_kernel_spmd(nc, in_maps, core_ids=list(range(n)))
    return np.stack([res.results[b]["out"] for b in range(B)], axis=0).astype(np.float32)
```
